# Optimizing a Trainium2 kernel written in Bass

```python
import math
import jax, jax.numpy as jnp
from jax import lax
import numpy as np

D_MODEL = 2048
BATCH = 2
SEQ = 16384
DEPTH = 1
DEC_BATCH = 32
DEC_SEQ = 64
PAST_LEN = 1024

CHUNK = 64
N_META = 16
Q_BLOCK = 128
GLA_HEADS = 4
GLA_DK = D_MODEL // (2 * GLA_HEADS)
GLA_DV = D_MODEL // GLA_HEADS
GATE_RANK = 16
GLA_TAU = 16.0
DIFF_HEADS = 8
DIFF_DQK = D_MODEL // (2 * DIFF_HEADS)
DIFF_DV = 2 * DIFF_DQK
ROPE_DIM = DIFF_DQK // 4
ROPE_THETA = 500000.0
D_FF = 4 * D_MODEL
EPS = 1e-6
SPLIT_SIZES = (GLA_HEADS * GLA_DK, GLA_HEADS * GLA_DK, GLA_HEADS * GLA_DV, GATE_RANK, GLA_HEADS * GLA_DV,
               2 * DIFF_HEADS * DIFF_DQK, 2 * DIFF_HEADS * DIFF_DQK, DIFF_HEADS * DIFF_DV, D_MODEL, D_MODEL)
IN_COLS = sum(SPLIT_SIZES)

kernel_name = 'hybrid_gla_diffattn_streaming_step'


def rmsnorm(x, g):
    xf = x.astype(jnp.float32)
    y = xf * lax.rsqrt(jnp.mean(jnp.square(xf), axis=-1, keepdims=True) + EPS)
    return (y * g.astype(jnp.float32)).astype(x.dtype)


def rope(x, pos):
    half = ROPE_DIM // 2
    inv_freq = jnp.power(ROPE_THETA, -jnp.arange(0, ROPE_DIM, 2, dtype=jnp.float32) / ROPE_DIM)
    ang = pos[:, None] * inv_freq[None, :]
    cos = jnp.cos(ang)[:, None, None, :]
    sin = jnp.sin(ang)[:, None, None, :]
    xf = x.astype(jnp.float32)
    x1 = xf[..., :half]
    x2 = xf[..., half:ROPE_DIM]
    out = jnp.concatenate([x1 * cos - x2 * sin, x2 * cos + x1 * sin, xf[..., ROPE_DIM:]], axis=-1)
    return out.astype(x.dtype)


def in_projection(u, lp):
    B, T = u.shape[:2]
    z = u @ lp['w_in']
    parts = []
    off = 0
    for size in SPLIT_SIZES:
        parts.append(z[..., off:off + size])
        off += size
    gq, gk, gv, alow, r, dq, dk, dv, ga, gb = parts
    gq = gq.reshape(B, T, GLA_HEADS, GLA_DK) * (GLA_DK ** -0.5)
    gk = gk.reshape(B, T, GLA_HEADS, GLA_DK)
    gv = gv.reshape(B, T, GLA_HEADS, GLA_DV)
    log_a = jax.nn.log_sigmoid((alow @ lp['w_gla_a2'] + lp['b_gla_a']).astype(jnp.float32)) / GLA_TAU
    log_a = log_a.reshape(B, T, GLA_HEADS, GLA_DK)
    dq = dq.reshape(B, T, DIFF_HEADS, 2, DIFF_DQK)
    dk = dk.reshape(B, T, DIFF_HEADS, 2, DIFF_DQK)
    dv = dv.reshape(B, T, DIFF_HEADS, DIFF_DV)
    return gq, gk, gv, log_a, r, dq, dk, dv, ga, gb


def gla_chunk(S, q, k, v, log_a):
    T = q.shape[1]
    b = jnp.cumsum(log_a, axis=1)
    qf = q.astype(jnp.float32)
    kf = k.astype(jnp.float32)
    vf = v.astype(jnp.float32)
    Sf = S.astype(jnp.float32)
    inter = jnp.einsum('bthc,bhcv->bthv', qf * jnp.exp(b), Sf)
    causal = jnp.tril(jnp.ones((T, T), dtype=bool))[None, :, :, None, None]
    rel = jnp.where(causal, b[:, :, None] - b[:, None], -jnp.inf)
    scores = jnp.einsum('bthc,bshc,btshc->bhts', qf, kf, jnp.exp(rel))
    intra = jnp.einsum('bhts,bshv->bthv', scores, vf)
    b_last = b[:, -1]
    k_dec = kf * jnp.exp(b_last[:, None] - b)
    S_new = jnp.exp(b_last)[..., None] * Sf + jnp.einsum('bshc,bshv->bhcv', k_dec, vf)
    return (inter + intra).astype(v.dtype), S_new.astype(S.dtype)


def diff_lambda(lp, lam_init):
    f = jnp.float32
    return (jnp.exp(jnp.sum(lp['diff_lq1'].astype(f) * lp['diff_lk1'].astype(f)))
            - jnp.exp(jnp.sum(lp['diff_lq2'].astype(f) * lp['diff_lk2'].astype(f))) + lam_init)


def diff_attend(q, k, v, mask, lam):
    s = jnp.einsum('bqhmd,bkhmd->bhmqk', q, k).astype(jnp.float32) * (DIFF_DQK ** -0.5)
    s = jnp.where(mask, s, -jnp.inf)
    p = jax.nn.softmax(s, axis=-1)
    pd = p[:, :, 0] - lam * p[:, :, 1]
    return jnp.einsum('bhqk,bkhv->bqhv', pd.astype(v.dtype), v)


def merge_and_ffn(x, o_gla, r, o_diff, ga, gb, lp, lam_init):
    B, T = x.shape[:2]
    a = rmsnorm(o_gla, lp['gla_norm']).reshape(B, T, GLA_HEADS * GLA_DV) * jax.nn.silu(r)
    d = (rmsnorm(o_diff, lp['diff_norm']) * (1.0 - lam_init)).reshape(B, T, DIFF_HEADS * DIFF_DV)
    mix = jax.nn.sigmoid(ga) * (a @ lp['w_br_gla']) + jax.nn.sigmoid(gb) * (d @ lp['w_br_diff'])
    h = x + rmsnorm(mix @ lp['w_o'], lp['norm_mix_post'])
    m = jnp.square(jax.nn.relu(rmsnorm(h, lp['norm_ffn_pre']) @ lp['w_ff1'])) @ lp['w_ff2']
    return h + rmsnorm(m, lp['norm_ffn_post'])


def prompt_layer(x, lp, lam_init):
    Bp, L = x.shape[:2]
    n_real = L - N_META
    u = rmsnorm(x, lp['norm_mix_pre'])
    gq, gk, gv, log_a, r, dq, dk, dv, ga, gb = in_projection(u, lp)
    S0 = jnp.zeros((Bp, GLA_HEADS, GLA_DK, GLA_DV), dtype=x.dtype)
    o_meta, S = gla_chunk(S0, gq[:, :N_META], gk[:, :N_META], gv[:, :N_META], log_a[:, :N_META])

    def to_chunks(t):
        return jnp.moveaxis(t[:, N_META:].reshape((Bp, n_real // CHUNK, CHUNK) + t.shape[2:]), 1, 0)

    def step(S_c, xs):
        o, S_n = gla_chunk(S_c, *xs)
        return S_n, o

    S_fin, o_real = lax.scan(step, S, (to_chunks(gq), to_chunks(gk), to_chunks(gv), to_chunks(log_a)))
    o_real = jnp.moveaxis(o_real, 0, 1).reshape(Bp, n_real, GLA_HEADS, GLA_DV)
    o_gla = jnp.concatenate([o_meta, o_real], axis=1)
    pos = jnp.arange(L, dtype=jnp.float32)
    dq = rope(dq, pos)
    dk = rope(dk, pos)
    lam = diff_lambda(lp, lam_init)
    o_dmeta = diff_attend(dq[:, :N_META], dk[:, :N_META], dv[:, :N_META],
                          jnp.ones((N_META, N_META), dtype=bool), lam)
    key_chunk = jnp.concatenate([-jnp.ones((N_META,), jnp.int32), jnp.arange(n_real, dtype=jnp.int32) // CHUNK])
    n_blk = n_real // Q_BLOCK
    q_blocks = jnp.moveaxis(dq[:, N_META:].reshape(Bp, n_blk, Q_BLOCK, DIFF_HEADS, 2, DIFF_DQK), 1, 0)

    def attend_block(args):
        qb, i = args
        q_chunk = (i * Q_BLOCK + jnp.arange(Q_BLOCK, dtype=jnp.int32)) // CHUNK
        mask = key_chunk[None, :] <= q_chunk[:, None]
        return diff_attend(qb, dk, dv, mask, lam)

    o_dreal = lax.map(attend_block, (q_blocks, jnp.arange(n_blk, dtype=jnp.int32)))
    o_dreal = jnp.moveaxis(o_dreal, 0, 1).reshape(Bp, n_real, DIFF_HEADS, DIFF_DV)
    o_diff = jnp.concatenate([o_dmeta, o_dreal], axis=1)
    y = merge_and_ffn(x, o_gla, r, o_diff, ga, gb, lp, lam_init)
    return y, dk, dv, S_fin


def sample_layer(x, cache_k_l, cache_v_l, state_l, lp, lam_init):
    T = x.shape[1]
    P = cache_k_l.shape[1]
    u = rmsnorm(x, lp['norm_mix_pre'])
    gq, gk, gv, log_a, r, dq, dk, dv, ga, gb = in_projection(u, lp)
    o_gla, S_new = gla_chunk(state_l, gq, gk, gv, log_a)
    pos = jnp.arange(T, dtype=jnp.float32) + P
    dq = rope(dq, pos)
    dk = rope(dk, pos)
    lam = diff_lambda(lp, lam_init)
    k_all = jnp.concatenate([cache_k_l.astype(dk.dtype), dk], axis=1)
    v_all = jnp.concatenate([cache_v_l.astype(dv.dtype), dv], axis=1)
    o_diff = diff_attend(dq, k_all, v_all, jnp.ones((T, P + T), dtype=bool), lam)
    y = merge_and_ffn(x, o_gla, r, o_diff, ga, gb, lp, lam_init)
    return y, dk, dv, S_new


def setup_inputs(seed: int = 0) -> dict:
    key = jax.random.key(seed)
    ks = jax.random.split(key, 24)
    f32 = jnp.float32

    def nrm(k, shape, scale=1.0):
        return jax.random.normal(k, shape, f32) * scale

    def gain(k, n):
        return 1.0 + 0.02 * jax.random.normal(k, (DEPTH, n), f32)

    return {
        'x_prompt': nrm(ks[0], (BATCH, SEQ, D_MODEL)),
        'x_sample': nrm(ks[1], (DEC_BATCH, DEC_SEQ, D_MODEL)),
        'cache_k': nrm(ks[2], (DEPTH, DEC_BATCH, PAST_LEN, DIFF_HEADS, 2, DIFF_DQK)),
        'cache_v': nrm(ks[3], (DEPTH, DEC_BATCH, PAST_LEN, DIFF_HEADS, DIFF_DV)),
        'state_gla': nrm(ks[4], (DEPTH, DEC_BATCH, GLA_HEADS, GLA_DK, GLA_DV), 0.5),
        'meta': nrm(ks[5], (N_META, D_MODEL)),
        'norm_mix_pre': gain(ks[6], D_MODEL),
        'w_in': nrm(ks[7], (DEPTH, D_MODEL, IN_COLS), D_MODEL ** -0.5),
        'w_gla_a2': nrm(ks[8], (DEPTH, GATE_RANK, GLA_HEADS * GLA_DK), GATE_RANK ** -0.5),
        'b_gla_a': nrm(ks[9], (DEPTH, GLA_HEADS * GLA_DK), 0.1),
        'gla_norm': gain(ks[10], GLA_DV),
        'diff_lq1': nrm(ks[11], (DEPTH, DIFF_DQK), 0.1),
        'diff_lk1': nrm(ks[12], (DEPTH, DIFF_DQK), 0.1),
        'diff_lq2': nrm(ks[13], (DEPTH, DIFF_DQK), 0.1),
        'diff_lk2': nrm(ks[14], (DEPTH, DIFF_DQK), 0.1),
        'diff_norm': gain(ks[15], DIFF_DV),
        'w_br_gla': nrm(ks[16], (DEPTH, GLA_HEADS * GLA_DV, D_MODEL), (GLA_HEADS * GLA_DV) ** -0.5),
        'w_br_diff': nrm(ks[17], (DEPTH, DIFF_HEADS * DIFF_DV, D_MODEL), (DIFF_HEADS * DIFF_DV) ** -0.5),
        'w_o': nrm(ks[18], (DEPTH, D_MODEL, D_MODEL), D_MODEL ** -0.5),
        'norm_mix_post': gain(ks[19], D_MODEL),
        'norm_ffn_pre': gain(ks[20], D_MODEL),
        'w_ff1': nrm(ks[21], (DEPTH, D_MODEL, D_FF), D_MODEL ** -0.5),
        'w_ff2': nrm(ks[22], (DEPTH, D_FF, D_MODEL), D_FF ** -0.5),
        'norm_ffn_post': gain(ks[23], D_MODEL),
    }


def reference(x_prompt, x_sample, cache_k, cache_v, state_gla, meta, norm_mix_pre, w_in, w_gla_a2,
              b_gla_a, gla_norm, diff_lq1, diff_lk1, diff_lq2, diff_lk2, diff_norm, w_br_gla, w_br_diff,
              w_o, norm_mix_post, norm_ffn_pre, w_ff1, w_ff2, norm_ffn_post):
    Bp = x_prompt.shape[0]
    hp = jnp.concatenate([jnp.broadcast_to(meta[None].astype(x_prompt.dtype), (Bp, N_META, D_MODEL)),
                          x_prompt], axis=1)
    hs = x_sample
    kp_list, vp_list, sp_list, ks_list, vs_list, ss_list = [], [], [], [], [], []
    for l in range(DEPTH):
        lp = {
            'norm_mix_pre': norm_mix_pre[l], 'w_in': w_in[l], 'w_gla_a2': w_gla_a2[l], 'b_gla_a': b_gla_a[l],
            'gla_norm': gla_norm[l], 'diff_lq1': diff_lq1[l], 'diff_lk1': diff_lk1[l],
            'diff_lq2': diff_lq2[l], 'diff_lk2': diff_lk2[l], 'diff_norm': diff_norm[l],
            'w_br_gla': w_br_gla[l], 'w_br_diff': w_br_diff[l], 'w_o': w_o[l],
            'norm_mix_post': norm_mix_post[l], 'norm_ffn_pre': norm_ffn_pre[l],
            'w_ff1': w_ff1[l], 'w_ff2': w_ff2[l], 'norm_ffn_post': norm_ffn_post[l],
        }
        lam_init = 0.8 - 0.6 * math.exp(-0.3 * l)
        hp, kp, vp, sp = prompt_layer(hp, lp, lam_init)
        hs, kn, vn, sn = sample_layer(hs, cache_k[l], cache_v[l], state_gla[l], lp, lam_init)
        kp_list.append(kp)
        vp_list.append(vp)
        sp_list.append(sp)
        ks_list.append(kn)
        vs_list.append(vn)
        ss_list.append(sn)
    y_prompt = hp[:, N_META:]
    y_sample = hs
    new_k_prompt = jnp.stack(kp_list, axis=0)
    new_v_prompt = jnp.stack(vp_list, axis=0)
    new_state_gla_prompt = jnp.stack(sp_list, axis=0)
    new_k_sample = jnp.stack(ks_list, axis=0)
    new_v_sample = jnp.stack(vs_list, axis=0)
    new_state_gla_sample = jnp.stack(ss_list, axis=0)
    return (y_prompt, y_sample, new_k_prompt, new_v_prompt, new_state_gla_prompt,
            new_k_sample, new_v_sample, new_state_gla_sample)
```

```python
import math
from contextlib import ExitStack
import numpy as np
import ml_dtypes
import concourse.bass as bass
import concourse.mybir as mybir
from concourse.bass_utils import run_bass_kernel_spmd

F32 = mybir.dt.float32
BF16 = mybir.dt.bfloat16
I32 = mybir.dt.int32
AF = mybir.ActivationFunctionType
ALU = mybir.AluOpType

D_MODEL = 2048
EPS = 1e-6
LAM_INIT = 0.2
SEM_SWITCH = 30000


class Buf:
    __slots__ = ("ap", "name", "small", "wr", "rd", "sem", "semval", "excl")

    def __init__(self, ap, name, small=False):
        self.ap = ap
        self.name = name
        self.small = small
        self.wr = None
        self.rd = []
        self.sem = None
        self.semval = 0
        self.excl = False

    def __getitem__(self, k):
        return self.ap[k]


class Prog:
    ENG = ("pe", "act", "dve", "pool", "sp")

    def __init__(self, nc, es, es_mem=None):
        self.nc = nc
        self.es = es
        self.es_mem = es_mem if es_mem is not None else es
        self.ops = {e: [] for e in self.ENG}
        self.cur_sem = {}
        self.cur_val = {}
        self.waited = {e: {} for e in self.ENG}
        self.nsem = 0
        self.nbuf = 0
        self.dmasems = []
        for e in self.ENG:
            self._new_sem(e)

    def new_sem(self, name):
        self.nsem += 1
        return self.es.enter_context(self.nc.semaphore(f"{name}_{self.nsem}"))

    def _new_sem(self, e):
        self.cur_sem[e] = self.new_sem("s" + e)
        self.cur_val[e] = 0

    def sb(self, name, shape, dtype, small=False):
        self.nbuf += 1
        t = self.es_mem.enter_context(self.nc.sbuf_tensor(f"{name}_{self.nbuf}", list(shape), dtype))
        return Buf(t, name, small)

    def ps(self, name, shape, dtype):
        self.nbuf += 1
        t = self.es_mem.enter_context(self.nc.psum_tensor(f"{name}_{self.nbuf}", list(shape), dtype))
        b = Buf(t, name)
        b.excl = True
        return b

    def _waits(self, eng, reads, writes, fence):
        deps = []
        for b in reads:
            if b.wr is not None:
                deps.append((b.wr, b.small))
        for b in writes:
            if b.wr is not None:
                deps.append((b.wr, b.small))
            for t in b.rd:
                deps.append((t, b.small))
        need = {}
        for (te, sem, val), small in deps:
            if te == eng and eng == "pe":
                continue
            key = id(sem)
            if self.waited[eng].get(key, 0) >= val:
                continue
            if key not in need or need[key][1] < val:
                need[key] = (sem, val)
        out = []
        for key, (sem, val) in need.items():
            self.waited[eng][key] = val
            out.append((sem, val))
        return out

    def op(self, eng, fn, reads=(), writes=(), fence=False):
        if eng != "pe":
            ex = [b for b in reads if b.excl]
            if ex:
                reads = tuple(b for b in reads if not b.excl)
                writes = tuple(writes) + tuple(b for b in ex if b not in writes)
        if self.cur_val[eng] >= SEM_SWITCH:
            self._new_sem(eng)
        waits = self._waits(eng, reads, writes, fence)
        self.cur_val[eng] += 1
        sem = self.cur_sem[eng]
        t = (eng, sem, self.cur_val[eng])
        self.ops[eng].append((waits, fn, (sem, 1)))
        for b in reads:
            b.rd.append(t)
        for b in writes:
            b.wr = t
            b.rd = []
        return t

    def dma(self, q, out_ap, in_ap, reads=(), writes=(), sembuf=None, **kw):
        sb_ = sembuf if sembuf is not None else (writes[0] if writes else reads[0])
        if sb_.sem is None:
            sb_.sem = {}
        if q not in sb_.sem:
            sb_.sem[q] = [self.new_sem("d" + sb_.name + q), 0]
            self.dmasems.append(sb_.sem[q])
        waits = self._waits(q, reads, writes, True)
        sb_.sem[q][1] += 16
        dsem = sb_.sem[q][0]
        t = ("dma", dsem, sb_.sem[q][1])

        def fn(e, out_ap=out_ap, in_ap=in_ap, kw=kw):
            return e.dma_start(out=out_ap, in_=in_ap, **kw)

        self.ops[q].append((waits, fn, (dsem, 16)))
        for b in reads:
            b.rd.append(t)
        for b in writes:
            b.wr = t
            b.rd = []
        return t

    def final_wait(self, eng, bufs):
        tickets = []
        for b in bufs:
            if b.wr is not None:
                tickets.append(b.wr)
            tickets.extend(b.rd)
        need = {}
        for (te, sem, val) in tickets:
            key = id(sem)
            if key not in need or need[key][1] < val:
                need[key] = (sem, val)
        self.ops[eng].append((list(need.values()), None, None))

    def final_all(self, eng):
        need = []
        for e in self.ENG:
            if self.cur_val[e] > 0:
                need.append((self.cur_sem[e], self.cur_val[e]))
        for sv in self.dmasems:
            need.append((sv[0], sv[1]))
        self.ops[eng].append((need, None, None))

    def emit(self, block):
        def mk(engname):
            def run(e):
                for waits, fn, inc in self.ops[engname]:
                    for sem, val in waits:
                        e.wait_ge(sem, val)
                    if fn is None:
                        continue
                    ins = fn(e)
                    if inc is not None:
                        ins.then_inc(inc[0], inc[1])
            return run

        block.tensor(mk("pe"))
        block.scalar(mk("act"))
        block.vector(mk("dve"))
        block.gpsimd(mk("pool"))
        block.sync(mk("sp"))


NCOL1 = 3088
C_GQ, C_GK, C_GV, C_R, C_DQ, C_DK, C_DV, C_AL = 0, 256, 512, 1024, 1536, 2048, 2560, 3072
NPT_FULL = 128
NSEQ_FULL = 16
NTOKCOL = 16384 + 1024


def build_phase1(nc, P, NPT=NPT_FULL, NSEQ=NSEQ_FULL, ad_kind="Internal"):
    LP = 16 + 128 * NPT
    dr = {}
    def din(name, shape, dt=F32):
        dr[name] = nc.dram_tensor(name, list(shape), dt, kind="ExternalInput").ap()
        return dr[name]
    def dout(name, shape, dt=F32, kind="ExternalOutput"):
        dr[name] = nc.dram_tensor(name, list(shape), dt, kind=kind).ap()
        return dr[name]
    xp = din("xp", [LP, 2048])
    xs = din("xs", [NSEQ * 64, 2048])
    w1 = din("w1", [2048, NCOL1])
    a2b = din("a2b", [17, 256])
    gnorm = din("gnorm", [1, 512])
    dnorm = din("dnorm", [1, 256])
    nmp = din("nmp", [128, 16])
    lql = din("lql", [1, 512])
    ck = din("ck", [NSEQ, 1024, 512])
    cv = din("cv", [NSEQ, 1024, 512])
    st_in = din("st_in", [NSEQ, 256, 512])
    ok = dout("ok", [LP, 512])
    ov = dout("ov", [LP, 512])
    oS = dout("oS", [256, 512])
    oks = dout("oks", [NSEQ * 64, 512])
    ovs = dout("ovs", [NSEQ * 64, 512])
    oSs = dout("oSs", [NSEQ, 256, 512])
    ad = dout("ad", [4, 8, 128, 4352], BF16, kind=ad_kind)
    NKB = NPT + 1
    kt_scr = nc.dram_tensor("kt_scr", [NKB, 128, 512], BF16).ap()
    v_scr = nc.dram_tensor("v_scr", [NKB, 128, 516], BF16).ap()
    cs_tab = nc.dram_tensor("cs_tab", [2, 128, 130, 16], F32).ap()
    kts_b = [Buf(None, f"kts{i}") for i in range(NKB)]
    vs_b = [Buf(None, f"vs{i}") for i in range(NKB)]

    Wb = P.sb("Wb", [128, 16, NCOL1], BF16)
    xts = [P.sb("xt", [128, 2048], F32) for _ in range(2)]
    xn = P.sb("xn", [128, 2048], BF16)
    uT = P.sb("uT", [128, 16, 128], BF16)
    junk = P.sb("junk", [128, 2048], BF16)
    st8 = [P.sb("st8", [128, 8], F32, small=True) for _ in range(8)]
    ident = P.sb("ident", [128, 128], BF16)
    cf = P.sb("cf", [128, 128], F32)
    Uneg = P.sb("Uneg", [128, 128], F32)
    U2neg = P.sb("U2neg", [128, 128], F32)
    causal = P.sb("causal", [128, 128], F32)
    negc = P.sb("negc", [128, 1], F32)
    a2b_sb = P.sb("a2b_sb", [17, 256], F32)
    gnormB = P.sb("gnormB", [128, 512], F32)
    dnormB = P.sb("dnormB", [128, 256], F32)
    nmp_sb = P.sb("nmp_sb", [128, 16], F32)
    lqB = P.sb("lqB", [128, 512], F32)
    lam = P.sb("lam", [128, 4], F32, small=True)
    alT = P.sb("alT", [17, 128], F32)
    e1 = P.sb("e1", [128, 256], F32)
    lnv = P.sb("lnv", [128, 256], F32)
    eb = P.sb("eb", [128, 256], F32)
    enb = P.sb("enb", [128, 256], F32)
    eD = P.sb("eD", [128, 256], F32)
    ebl = P.sb("ebl", [128, 2], F32, small=True)
    qt = P.sb("qt", [128, 256], BF16)
    ktl = P.sb("ktl", [128, 256], BF16)
    kdec = P.sb("kdec", [128, 256], BF16)
    qkT = P.sb("qkT", [128, 4, 128], BF16)
    gvb = P.sb("gvb", [128, 512], BF16)
    scT = P.sb("scT", [128, 128], BF16)
    S = P.sb("S", [128, 2, 512], F32)
    Sbf = P.sb("Sbf", [128, 2, 512], BF16)
    er = P.sb("er", [128, 512], F32)
    rsil = P.sb("rsil", [128, 512], F32)
    t1 = P.sb("t1", [128, 512], F32)
    a_sb = P.sb("a_sb", [128, 512], BF16)
    aT = P.sb("aT", [128, 4, 128], BF16)
    cs = [P.sb("cs", [128, 32], F32) for _ in range(2)]
    rt = [P.sb("rt", [128, 4, 16], F32) for _ in range(4)]
    kf = [P.sb("kf", [128, 512], F32) for _ in range(2)]
    kb = P.sb("kb", [128, 512], BF16)
    qb = P.sb("qb", [128, 512], BF16)
    qT = P.sb("qT", [128, 4, 128], BF16)
    vf = [P.sb("vf", [128, 512], F32) for _ in range(2)]
    KTc = [P.sb("KTc", [128, 4, 128], BF16) for _ in range(2)]
    Vc = [P.sb("Vc", [128, 2, 258], BF16) for _ in range(2)]
    KTm = P.sb("KTm", [128, 4, 16], BF16)
    Vm = P.sb("Vm", [128, 2, 258], BF16)
    KTg = [P.sb("KTg", [128, 4, 512], BF16) for _ in range(2)]
    Vg = [P.sb("Vg", [128, 4, 516], BF16) for _ in range(2)]
    pT = [P.sb("pT", [128, 4, 128], BF16) for _ in range(2)]
    Osb = P.sb("Osb", [128, 4, 257], F32)
    rz = P.sb("rz", [128, 8], F32, small=True)
    od = P.sb("od", [128, 2, 256], F32)
    tq = P.sb("tq", [128, 256], F32)
    d_sb = P.sb("d_sb", [128, 512], BF16)
    dT = P.sb("dT", [128, 4, 128], BF16)
    ckb = P.sb("ckb", [128, 4, 512], BF16)

    banks = [P.ps("bk", [128, 512], F32) for _ in range(8)]
    NBK = 8
    state = {"bk": 0, "tb": 0, "st": 0, "ev": 0}

    def nbank():
        b = banks[state["bk"] % NBK]
        state["bk"] += 1
        return b

    def tbv(b):
        return b.ap[:, :].bitcast(BF16)

    def nst():
        b = st8[state["st"] % 8]
        state["st"] += 1
        return b

    def evac_eng():
        state["ev"] += 1
        return "act" if state["ev"] % 2 else "dve"

    def copy(eng, out_ap, in_ap, reads, writes):
        if eng == "act":
            return P.op("act", lambda e: e.copy(out=out_ap, in_=in_ap), reads, writes)
        if eng == "dve":
            return P.op("dve", lambda e: e.tensor_copy(out=out_ap, in_=in_ap), reads, writes)
        return P.op("pool", lambda e: e.tensor_copy(out=out_ap, in_=in_ap), reads, writes)

    def mk_mask(dst, fillv, pattern, cm, cmp, fill):
        P.op("pool", lambda e: e.memset(dst.ap[:], fillv), (), (dst,))
        P.op("pool", lambda e: e.affine_select(out=dst.ap[:], in_=dst.ap[:], pattern=pattern, compare_op=cmp,
                                              fill=fill, base=0, channel_multiplier=cm), (dst,), (dst,), fence=True)
    mk_mask(cf, 0.0, [[-1, 128]], 1, ALU.not_equal, 1.0)
    P.op("pool", lambda e: e.tensor_copy(out=ident.ap[:], in_=cf.ap[:]), (cf,), (ident,), fence=True)
    mk_mask(Uneg, -1.0 / 16, [[1, 128]], -1, ALU.is_ge, 0.0)
    mk_mask(U2neg, -1.0 / 16, [[-1, 128]], 1, ALU.is_gt, 0.0)
    mk_mask(causal, 1.0, [[1, 128]], -1, ALU.is_ge, 0.0)
    P.op("pool", lambda e: e.memset(negc.ap[:], -1.0 / 16), (), (negc,))
    P.op("pool", lambda e: e.memset(alT.ap[:], 1.0), (), (alT,))
    for v_ in Vc:
        P.op("pool", lambda e, v_=v_: e.memset(v_.ap[:], 1.0), (), (v_,))
    P.op("pool", lambda e: e.memset(Vm.ap[:], 1.0), (), (Vm,))
    for v_ in Vg:
        P.op("pool", lambda e, v_=v_: e.memset(v_.ap[:], 1.0), (), (v_,))
    P.op("pool", lambda e: e.memset(S.ap[:], 0.0), (), (S,))
    P.op("pool", lambda e: e.memset(Sbf.ap[:], 0.0), (), (Sbf,))
    P.dma("sp", a2b_sb.ap[:], a2b[:, :], (), (a2b_sb,))
    P.dma("sp", gnormB.ap[:], gnorm.partition_broadcast(128), (), (gnormB,))
    P.dma("sp", dnormB.ap[:], dnorm.partition_broadcast(128), (), (dnormB,))
    P.dma("sp", nmp_sb.ap[:], nmp[:, :], (), (nmp_sb,))
    P.dma("sp", lqB.ap[:], lql.partition_broadcast(128), (), (lqB,))
    P.op("dve", lambda e: e.tensor_scalar_mul(out=dnormB.ap[:], in0=dnormB.ap[:], scalar1=1.0 - LAM_INIT), (dnormB,), (dnormB,))
    P.op("dve", lambda e: e.tensor_tensor(out=lqB.ap[:, 0:128], in0=lqB.ap[:, 0:128], in1=lqB.ap[:, 128:256], op=ALU.mult), (lqB,), (lqB,))
    P.op("dve", lambda e: e.tensor_tensor(out=lqB.ap[:, 256:384], in0=lqB.ap[:, 256:384], in1=lqB.ap[:, 384:512], op=ALU.mult), (lqB,), (lqB,))
    P.op("dve", lambda e: e.tensor_reduce(out=lam.ap[:, 0:1], in_=lqB.ap[:, 0:128], axis=mybir.AxisListType.X, op=ALU.add), (lqB,), (lam,), fence=True)
    P.op("dve", lambda e: e.tensor_reduce(out=lam.ap[:, 1:2], in_=lqB.ap[:, 256:384], axis=mybir.AxisListType.X, op=ALU.add), (lqB,), (lam,), fence=True)
    P.op("act", lambda e: e.activation(out=lam.ap[:, 0:2], in_=lam.ap[:, 0:2], func=AF.Exp), (lam,), (lam,))
    P.op("dve", lambda e: e.tensor_tensor(out=lam.ap[:, 2:3], in0=lam.ap[:, 1:2], in1=lam.ap[:, 0:1], op=ALU.subtract), (lam,), (lam,))
    P.op("dve", lambda e: e.tensor_scalar_add(out=lam.ap[:, 2:3], in0=lam.ap[:, 2:3], scalar1=-LAM_INIT), (lam,), (lam,))

    import os
    STOP = int(os.environ.get("MK_STOP", "99"))
    if STOP <= 1:
        return dr, [a2b_sb, gnormB, dnormB, nmp_sb, lqB, lam]
    posi = P.sb("posi", [128, 130], I32)
    posf = P.sb("posf", [128, 130], F32)
    invf = P.sb("invf", [128, 16], F32)
    P.op("pool", lambda e: e.iota(posi.ap[:, 0:1], pattern=[[0, 1]], base=0, channel_multiplier=1), (), (posi,))
    P.op("pool", lambda e: e.iota(posi.ap[:, 1:129], pattern=[[128, 128]], base=16, channel_multiplier=1), (), (posi,))
    P.op("pool", lambda e: e.iota(posi.ap[:, 129:130], pattern=[[0, 1]], base=1024, channel_multiplier=1), (), (posi,))
    P.op("pool", lambda e: e.tensor_copy(out=posf.ap[:], in_=posi.ap[:]), (posi,), (posf,), fence=True)
    for i in range(16):
        P.op("pool", lambda e, i=i: e.memset(invf.ap[:, i:i + 1], float(500000.0 ** (-(2.0 * i) / 32.0))), (), (invf,))
    PI = math.pi
    tab_dram = Buf(None, "tabd")
    C1 = 6.28125
    C2 = 2 * PI - C1
    def mk_tab(posb, pos_ap, n, Ab, A, Bb, Bp, stores):
        KIb, KFb = KTg[0], KTg[1]
        KI = KIb.ap[:].rearrange("p a b -> p (a b)").bitcast(I32)[:, 0:n * 16].rearrange("p (i c) -> p i c", c=16)
        KF = KFb.ap[:].rearrange("p a b -> p (a b)").bitcast(F32)[:, 0:n * 16].rearrange("p (i c) -> p i c", c=16)
        P.op("dve", lambda e: e.tensor_tensor(out=A, in0=pos_ap.unsqueeze(2).to_broadcast([128, n, 16]),
                                             in1=invf.ap[:].unsqueeze(1).to_broadcast([128, n, 16]), op=ALU.mult), (posb, invf), (Ab,))
        P.op("dve", lambda e: e.tensor_scalar(out=Bp, in0=A, scalar1=1.0 / (2 * PI), scalar2=None, op0=ALU.mult), (Ab,), (Bb,), fence=True)
        P.op("dve", lambda e: e.tensor_copy(out=KI, in_=Bp), (Bb,), (KIb,), fence=True)
        P.op("dve", lambda e: e.tensor_copy(out=KF, in_=KI), (KIb,), (KFb,), fence=True)
        P.op("dve", lambda e: e.scalar_tensor_tensor(out=Bp, in0=KF, scalar=-C1, in1=A, op0=ALU.mult, op1=ALU.add), (KFb, Ab), (Bb,), fence=True)
        P.op("dve", lambda e: e.scalar_tensor_tensor(out=Bp, in0=KF, scalar=-C2, in1=Bp, op0=ALU.mult, op1=ALU.add), (KFb, Bb), (Bb,), fence=True)
        P.op("dve", lambda e: e.tensor_single_scalar(out=KF, in_=Bp, scalar=PI, op=ALU.is_gt), (Bb,), (KFb,), fence=True)
        P.op("dve", lambda e: e.scalar_tensor_tensor(out=Bp, in0=KF, scalar=-2 * PI, in1=Bp, op0=ALU.mult, op1=ALU.add), (KFb, Bb), (Bb,), fence=True)
        P.op("dve", lambda e: e.tensor_scalar(out=Bp, in0=Bp, scalar1=0.999999, scalar2=None, op0=ALU.mult), (Bb,), (Bb,), fence=True)
        P.op("act", lambda e: e.activation(out=A, in_=Bp, func=AF.Sin, scale=0.5), (Bb,), (Ab,))
        P.op("act", lambda e: e.activation(out=Bp, in_=Bp, func=AF.Sin), (Bb,), (Bb,))
        P.op("dve", lambda e: e.tensor_tensor(out=A, in0=A, in1=A, op=ALU.mult), (Ab,), (Ab,), fence=True)
        P.op("dve", lambda e: e.tensor_scalar(out=A, in0=A, scalar1=-2.0, scalar2=1.0, op0=ALU.mult, op1=ALU.add), (Ab,), (Ab,), fence=True)
        for (a_, ap_, b_) in ((0, A, Ab), (1, Bp, Bb)):
            for (lo, hi, dlo) in stores:
                P.dma("sp", cs_tab[a_, :, dlo:dlo + (hi - lo), :], ap_[:, lo:hi, :], (b_,), (tab_dram,), sembuf=b_)
    for half in range(2):
        A = xts[0].ap[:, half * 1024:(half + 1) * 1024].rearrange("p (i c) -> p i c", c=16)
        Bp = xts[1].ap[:, half * 1024:(half + 1) * 1024].rearrange("p (i c) -> p i c", c=16)
        mk_tab(posf, posf.ap[:, 1 + 64 * half:65 + 64 * half], 64, xts[0], A, xts[1], Bp, [(0, 64, 1 + 64 * half)])
    posS = P.sb("posS", [128, 2], F32)
    P.op("pool", lambda e: e.tensor_copy(out=posS.ap[:, 0:1], in_=posf.ap[:, 0:1]), (posf,), (posS,))
    P.op("pool", lambda e: e.tensor_copy(out=posS.ap[:, 1:2], in_=posf.ap[:, 129:130]), (posf,), (posS,))
    tSa = P.sb("tSa", [128, 2, 16], F32)
    tSb = P.sb("tSb", [128, 2, 16], F32)
    mk_tab(posS, posS.ap[:, :], 2, tSa, tSa.ap[:], tSb, tSb.ap[:], [(0, 1, 0), (1, 2, 129)])

    if STOP <= 2:
        return dr, [xts[0], xts[1], tSa, tSb, lam]
    for c in range(16):
        for (lo, hi) in ((0, 2048), (2048, NCOL1)):
            xt_ = xts[(2 * c + (lo > 0)) % 2]
            P.dma("sp", xt_.ap[:, 0:hi - lo], w1[c * 128:(c + 1) * 128, lo:hi], (), (xt_,))
            eng = "dve" if lo == 0 else "pool"
            P.op(eng, lambda e, c=c, lo=lo, hi=hi, xt_=xt_: e.tensor_scalar(out=Wb.ap[:, c, lo:hi], in0=xt_.ap[:, 0:hi - lo],
                                                                          scalar1=nmp_sb.ap[:, c:c + 1], scalar2=None, op0=ALU.mult),
                 (xt_, nmp_sb), (Wb,))

    if STOP <= 3:
        return dr, [xts[0], xts[1], tSa, tSb, lam, Wb]
    def rstd_chain(ssb, col, n, scale):
        P.op("act", lambda e: e.activation(out=ssb.ap[:, col:col + n], in_=ssb.ap[:, col:col + n], func=AF.Ln, scale=scale, bias=EPS), (ssb,), (ssb,))
        P.op("act", lambda e: e.activation(out=ssb.ap[:, col:col + n], in_=ssb.ap[:, col:col + n], func=AF.Exp, scale=-0.5), (ssb,), (ssb,))

    def transposes(src, T, ncol_chunks, dst, dst_sl):
        j = 0
        while j < ncol_chunks:
            n = min(8, ncol_chunks - j)
            tb = nbank()
            tv = tbv(tb)
            def fn(e, j=j, n=n, tv=tv):
                ins = None
                for q in range(n):
                    ins = e.transpose(out=tv[:, q * 128:q * 128 + T], in_=src.ap[:T, (j + q) * 128:(j + q + 1) * 128], identity=ident.ap[:T, :T])
                return ins
            P.op("pe", fn, (src, ident), (tb,))
            eng = evac_eng()
            out_ap = dst.ap[:, dst_sl(j, n), 0:T]
            in_ap = tv[:, 0:n * 128].rearrange("p (q t) -> p q t", q=n)[:, :, 0:T]
            copy(eng, out_ap, in_ap, (tb,), (dst,))
            j += n

    def proj(T, c0, ncols, bank, uTb=uT):
        def fn(e):
            ins = None
            for c in range(16):
                ins = e.matmul(bank.ap[:T, 0:ncols], lhsT=uTb.ap[:, c, 0:T], rhs=Wb.ap[:, c, c0:c0 + ncols], start=(c == 0), stop=(c == 15))
            return ins
        P.op("pe", fn, (uTb, Wb), (bank,))

    def tile(idx, kind, T, xsrc, tabidx, krows, vrows, adcol, kbidx, seq=None):
        xt = xts[idx % 2]
        ss = nst()
        P.dma("sp", xt.ap[:T, :], xsrc, (), (xt,))
        csb = cs[idx % 2]
        P.dma("sp", csb.ap[:T, :].rearrange("p (a c) -> p a c", a=2), cs_tab[:, 0:T, tabidx, :].rearrange("a p c -> p a c"), (tab_dram,), (csb,))
        P.op("act", lambda e: e.memzero(ss.ap[:]), (), (ss,))
        P.op("act", lambda e: e.activation(out=junk.ap[:T, :], in_=xt.ap[:T, :], func=AF.Square, accum_out=ss.ap[:T, 0:1]), (xt,), (junk, ss))
        rstd_chain(ss, 0, 1, 1.0 / 2048)
        P.op("dve", lambda e: e.tensor_scalar(out=xn.ap[:T, :], in0=xt.ap[:T, :], scalar1=ss.ap[:T, 0:1], scalar2=None, op0=ALU.mult), (xt, ss), (xn,))
        transposes(xn, T, 16, uT, lambda j, n: slice(j, j + n))
        if STOP == 4: raise StopIteration
        bal = nbank()
        def fn_al(e):
            ins = None
            for c in range(16):
                ins = e.matmul(bal.ap[0:16, 0:T], lhsT=Wb.ap[:, c, C_AL:C_AL + 16], rhs=uT.ap[:, c, 0:T], start=(c == 0), stop=(c == 15))
            return ins
        P.op("pe", fn_al, (uT, Wb), (bal,))
        P.op("dve", lambda e: e.tensor_copy(out=alT.ap[0:16, 0:T], in_=bal.ap[0:16, 0:T]), (bal,), (alT,))
        bla = nbank()
        P.op("pe", lambda e: e.matmul(bla.ap[:T, 0:256], lhsT=alT.ap[0:17, 0:T], rhs=a2b_sb.ap[0:17, :], start=True, stop=True), (alT, a2b_sb), (bla,))
        P.op("act", lambda e: e.activation(out=e1.ap[:T, :], in_=bla.ap[:T, 0:256], func=AF.Exp, scale=-1.0), (bla,), (e1,))
        P.op("act", lambda e: e.activation(out=lnv.ap[:T, :], in_=e1.ap[:T, :], func=AF.Ln, bias=1.0), (e1,), (lnv,))
        bbd = nbank()
        def fn_bd(e):
            e.matmul(bbd.ap[:T, 0:256], lhsT=Uneg.ap[:T, :T], rhs=lnv.ap[:T, :], start=True, stop=True)
            return e.matmul(bbd.ap[:T, 256:512], lhsT=U2neg.ap[:T, :T], rhs=lnv.ap[:T, :], start=True, stop=True)
        P.op("pe", fn_bd, (Uneg, U2neg, lnv), (bbd,))
        bbl = nbank()
        def fn_bl(e):
            e.matmul(bbl.ap[:, 0:1], lhsT=lnv.ap[:T, 0:128], rhs=negc.ap[:T, 0:1], start=True, stop=True)
            return e.matmul(bbl.ap[:, 1:2], lhsT=lnv.ap[:T, 128:256], rhs=negc.ap[:T, 0:1], start=True, stop=True)
        P.op("pe", fn_bl, (lnv, negc), (bbl,))
        P.op("act", lambda e: e.activation(out=eb.ap[:T, :], in_=bbd.ap[:T, 0:256], func=AF.Exp), (bbd,), (eb,))
        P.op("act", lambda e: e.activation(out=enb.ap[:T, :], in_=bbd.ap[:T, 0:256], func=AF.Exp, scale=-1.0), (bbd,), (enb,))
        P.op("act", lambda e: e.activation(out=eD.ap[:T, :], in_=bbd.ap[:T, 256:512], func=AF.Exp), (bbd,), (eD,))
        P.op("act", lambda e: e.activation(out=ebl.ap[:, 0:2], in_=bbl.ap[:, 0:2], func=AF.Exp), (bbl,), (ebl,))
        bqk = nbank()
        proj(T, C_GQ, 512, bqk)
        P.op("dve", lambda e: e.scalar_tensor_tensor(out=qt.ap[:T, :], in0=bqk.ap[:T, 0:256], scalar=0.0625, in1=eb.ap[:T, :], op0=ALU.mult, op1=ALU.mult), (bqk, eb), (qt,))
        P.op("dve", lambda e: e.tensor_tensor(out=ktl.ap[:T, :], in0=bqk.ap[:T, 256:512], in1=enb.ap[:T, :], op=ALU.mult), (bqk, enb), (ktl,))
        P.op("dve", lambda e: e.tensor_tensor(out=kdec.ap[:T, :], in0=bqk.ap[:T, 256:512], in1=eD.ap[:T, :], op=ALU.mult), (bqk, eD), (kdec,))
        bgv = nbank()
        proj(T, C_GV, 512, bgv)
        P.op("act", lambda e: e.copy(out=gvb.ap[:T, :], in_=bgv.ap[:T, :]), (bgv,), (gvb,))
        br = nbank()
        proj(T, C_R, 512, br)
        P.op("act", lambda e: e.activation(out=er.ap[:T, :], in_=br.ap[:T, :], func=AF.Exp, scale=-1.0), (br,), (er,))
        P.op("dve", lambda e: e.tensor_scalar_add(out=er.ap[:T, :], in0=er.ap[:T, :], scalar1=1.0), (er,), (er,))
        P.op("dve", lambda e: e.reciprocal(out=er.ap[:T, :], in_=er.ap[:T, :]), (er,), (er,))
        P.op("dve", lambda e: e.tensor_tensor(out=rsil.ap[:T, :], in0=br.ap[:T, :], in1=er.ap[:T, :], op=ALU.mult), (br, er), (rsil,))
        transposes(qt, T, 2, qkT, lambda j, n: slice(j, j + n))
        transposes(ktl, T, 2, qkT, lambda j, n: slice(2 + j, 2 + j + n))
        bsc = nbank()
        def fn_sc(e):
            e.matmul(bsc.ap[:T, 0:T], lhsT=qkT.ap[:, 2, 0:T], rhs=qkT.ap[:, 0, 0:T], start=True, stop=False)
            return e.matmul(bsc.ap[:T, 0:T], lhsT=qkT.ap[:, 3, 0:T], rhs=qkT.ap[:, 1, 0:T], start=False, stop=True)
        P.op("pe", fn_sc, (qkT,), (bsc,))
        P.op("dve", lambda e: e.tensor_tensor(out=scT.ap[:T, :T], in0=bsc.ap[:T, 0:T], in1=causal.ap[:T, :T], op=ALU.mult), (bsc, causal), (scT,))
        bo = nbank()
        def fn_o(e):
            e.matmul(bo.ap[:T, :], lhsT=qkT.ap[:, 0, 0:T], rhs=Sbf.ap[:, 0, :], start=True, stop=False)
            e.matmul(bo.ap[:T, :], lhsT=qkT.ap[:, 1, 0:T], rhs=Sbf.ap[:, 1, :], start=False, stop=False)
            return e.matmul(bo.ap[:T, :], lhsT=scT.ap[:T, :T], rhs=gvb.ap[:T, :], start=False, stop=True)
        P.op("pe", fn_o, (qkT, Sbf, scT, gvb), (bo,))
        for ci in range(2):
            bds = nbank()
            P.op("pe", lambda e, ci=ci, bds=bds: e.matmul(bds.ap[:, :], lhsT=kdec.ap[:T, ci * 128:(ci + 1) * 128], rhs=gvb.ap[:T, :], start=True, stop=True), (kdec, gvb), (bds,))
            P.op("dve", lambda e, ci=ci, bds=bds: e.scalar_tensor_tensor(out=S.ap[:, ci, :], in0=S.ap[:, ci, :], scalar=ebl.ap[:, ci:ci + 1], in1=bds.ap[:, :],
                                                                      op0=ALU.mult, op1=ALU.add), (S, ebl, bds), (S,))
        P.op("pool", lambda e: e.tensor_copy(out=Sbf.ap[:], in_=S.ap[:]), (S,), (Sbf,))
        sg = nst()
        P.op("act", lambda e: e.memzero(sg.ap[:]), (), (sg,))
        P.op("act", lambda e: e.activation(out=junk.ap[:T, 0:512], in_=bo.ap[:T, :], func=AF.Square, accum_out=sg.ap[:T, 0:1]), (bo,), (junk, sg))
        rstd_chain(sg, 0, 1, 1.0 / 512)
        P.op("dve", lambda e: e.scalar_tensor_tensor(out=t1.ap[:T, :], in0=bo.ap[:T, :], scalar=sg.ap[:T, 0:1], in1=gnormB.ap[:T, :], op0=ALU.mult, op1=ALU.mult), (bo, sg, gnormB), (t1,))
        P.op("dve", lambda e: e.tensor_tensor(out=a_sb.ap[:T, :], in0=t1.ap[:T, :], in1=rsil.ap[:T, :], op=ALU.mult), (t1, rsil), (a_sb,))
        if adcol is not None:
            transposes(a_sb, T, 4, aT, lambda j, n: slice(j, j + n))
            P.dma("pool", ad[adcol[0], 0:4, :, adcol[1]:adcol[1] + T].rearrange("c p t -> p c t"), aT.ap[:, :, 0:T], (aT,), (), sembuf=aT)
        if STOP == 5: raise StopIteration
        bdq = nbank()
        proj(T, C_DQ, 512, bdq)
        bdk = nbank()
        proj(T, C_DK, 512, bdk)
        bdv = nbank()
        proj(T, C_DV, 512, bdv)
        if STOP == 51: raise StopIteration
        kfb = kf[idx % 2]
        vfb = vf[idx % 2]
        cosb = csb.ap[:T, 0:16].unsqueeze(1).to_broadcast([T, 4, 16])
        sinb = csb.ap[:T, 16:32].unsqueeze(1).to_broadcast([T, 4, 16])
        for (bsrc, dst) in ((bdq, qb), (bdk, kfb)):
            s3 = bsrc.ap[:T, :].rearrange("p (g d) -> p g d", g=4)
            d3 = dst.ap[:T, :].rearrange("p (g d) -> p g d", g=4)
            x1 = s3[:, :, 0:16]
            x2 = s3[:, :, 16:32]
            P.op("dve", lambda e, x1=x1: e.tensor_tensor(out=rt[0].ap[:T], in0=x1, in1=cosb, op=ALU.mult), (bsrc, csb), (rt[0],))
            P.op("dve", lambda e, x2=x2: e.tensor_tensor(out=rt[1].ap[:T], in0=x2, in1=sinb, op=ALU.mult), (bsrc, csb), (rt[1],))
            P.op("dve", lambda e, x2=x2: e.tensor_tensor(out=rt[2].ap[:T], in0=x2, in1=cosb, op=ALU.mult), (bsrc, csb), (rt[2],))
            P.op("dve", lambda e, x1=x1: e.tensor_tensor(out=rt[3].ap[:T], in0=x1, in1=sinb, op=ALU.mult), (bsrc, csb), (rt[3],))
            P.op("dve", lambda e, d3=d3: e.tensor_tensor(out=d3[:, :, 0:16], in0=rt[0].ap[:T], in1=rt[1].ap[:T], op=ALU.subtract), (rt[0], rt[1]), (dst,), fence=True)
            P.op("dve", lambda e, d3=d3: e.tensor_tensor(out=d3[:, :, 16:32], in0=rt[2].ap[:T], in1=rt[3].ap[:T], op=ALU.add), (rt[2], rt[3]), (dst,), fence=True)
            P.op("act", lambda e, d3=d3, s3=s3: e.copy(out=d3[:, :, 32:128], in_=s3[:, :, 32:128]), (bsrc,), (dst,))
        if STOP == 52: raise StopIteration
        P.op("pool", lambda e: e.tensor_copy(out=kb.ap[:T, :], in_=kfb.ap[:T, :]), (kfb,), (kb,))
        P.dma("sp", krows, kfb.ap[:T, :], (kfb,), (), sembuf=kfb)
        if STOP == 53: raise StopIteration
        P.op("act", lambda e: e.copy(out=vfb.ap[:T, :], in_=bdv.ap[:T, :]), (bdv,), (vfb,))
        P.dma("sp", vrows, vfb.ap[:T, :], (vfb,), (), sembuf=vfb)
        if STOP == 531: raise StopIteration
        if kind == "meta":
            KT_own, V_own = KTm, Vm
        else:
            KT_own, V_own = KTc[idx % 2], Vc[idx % 2]
        for h in range(2):
            P.op("act", lambda e, h=h: e.copy(out=V_own.ap[:T, h, 0:256], in_=bdv.ap[:T, h * 256:(h + 1) * 256]), (bdv,), (V_own,))
        if STOP == 54: raise StopIteration
        transposes(qb, T, 4, qT, lambda j, n: slice(j, j + n))
        if STOP == 55: raise StopIteration
        transposes(kb, T, 4, KT_own, lambda j, n: slice(j, j + n))
        if kind == "prompt":
            P.dma("pool", kt_scr[kbidx].rearrange("p (h t) -> p h t", h=4), KT_own.ap[:, :, :], (KT_own,), (kts_b[kbidx],), sembuf=KT_own)
            P.dma("pool", v_scr[kbidx].rearrange("p (h v) -> p h v", h=2), V_own.ap[:, :, :], (V_own,), (vs_b[kbidx],), sembuf=V_own)
        if STOP == 6: raise StopIteration
        blocks = []
        loaders = {}
        first_of_group = []
        if kind == "meta":
            blocks.append((KTm, lambda hm: KTm.ap[:, hm, 0:16], Vm, lambda h: Vm.ap[0:16, h, 0:257], 16, False))
        elif kind == "prompt":
            blocks.append((KTm, lambda hm: KTm.ap[:, hm, 0:16], Vm, lambda h: Vm.ap[0:16, h, 0:257], 16, False))
            nprev = kbidx - 1
            b0 = 1
            while b0 <= nprev:
                nb = min(4, nprev - b0 + 1)
                slot = state.setdefault("kg", 0) % 2
                state["kg"] += 1
                ktg, vg = KTg[slot], Vg[slot]
                def loader(ktg=ktg, vg=vg, b0=b0, nb=nb):
                    P.dma("sp", ktg.ap[:, 0:nb, :], kt_scr[b0:b0 + nb].rearrange("b p c -> p b c"), tuple(kts_b[b0:b0 + nb]), (ktg,))
                    P.dma("sp", vg.ap[:, 0:nb, :], v_scr[b0:b0 + nb].rearrange("b p c -> p b c"), tuple(vs_b[b0:b0 + nb]), (vg,))
                for q in range(nb):
                    blocks.append((ktg, (lambda hm, ktg=ktg, q=q: ktg.ap[:, q, hm * 128:(hm + 1) * 128]),
                                   vg, (lambda h, vg=vg, q=q: vg.ap[:, q, h * 258:h * 258 + 257]), 128, False))
                    loaders[len(blocks) - 1] = None
                first_of_group.append((len(blocks) - nb, loader))
                b0 += nb
            blocks.append((KT_own, lambda hm: KT_own.ap[:, hm, 0:128], V_own, lambda h: V_own.ap[:, h, 0:257], 128, True))
        else:
            for q in range(8):
                blocks.append((KTg[q // 4], (lambda hm, q=q: KTg[q // 4].ap[:, q % 4, hm * 128:(hm + 1) * 128]),
                               Vg[q // 4], (lambda h, q=q: Vg[q // 4].ap[:, q % 4, h * 258:h * 258 + 257]), 128, False))
            blocks.append((KT_own, lambda hm: KT_own.ap[:, hm, 0:T], V_own, lambda h: V_own.ap[:T, h, 0:257], T, False))
        A = [nbank() for _ in range(4)]
        nblk = len(blocks)
        gstart = {bi_: gi_ for gi_, (bi_, _) in enumerate(first_of_group)}
        for gi_ in range(min(2, len(first_of_group))):
            first_of_group[gi_][1]()
        for bi, (ktbuf, ktf, vbuf, vf_, Sk, diag) in enumerate(blocks):
            if bi in gstart and gstart[bi] >= 1 and gstart[bi] + 1 < len(first_of_group):
                first_of_group[gstart[bi] + 1][1]()
            bs = nbank()
            while bs in A:
                bs = nbank()
            def fn_qk(e, ktf=ktf, Sk=Sk, bs=bs):
                ins = None
                for hm in range(4):
                    ins = e.matmul(bs.ap[:Sk, hm * 128:hm * 128 + T], lhsT=ktf(hm), rhs=qT.ap[:, hm, 0:T], start=True, stop=True)
                return ins
            P.op("pe", fn_qk, (ktbuf, qT), (bs,))
            pt = pT[bi % 2]
            P.op("act", lambda e, bs=bs, pt=pt, Sk=Sk: e.activation(out=pt.ap[:Sk, :, 0:T], in_=bs.ap[:Sk, :].rearrange("p (h t) -> p h t", h=4)[:, :, 0:T],
                                                               func=AF.Exp, scale=128.0 ** -0.5), (bs,), (pt,))
            if diag:
                P.op("pool", lambda e, pt=pt: e.memset(pt.ap[64:128, :, 0:64], 0.0), (pt,), (pt,))
            def fn_pv(e, pt=pt, vf_=vf_, Sk=Sk, bi=bi):
                ins = None
                for hm in range(4):
                    ins = e.matmul(A[hm].ap[:T, 0:257], lhsT=pt.ap[:Sk, hm, 0:T], rhs=vf_(hm // 2), start=(bi == 0), stop=(bi == nblk - 1))
                return ins
            P.op("pe", fn_pv, (pt, vbuf), tuple(A))
        if STOP == 7: raise StopIteration
        for hm in range(4):
            copy("act" if hm % 2 else "dve", Osb.ap[:T, hm, :], A[hm].ap[:T, 0:257], (A[hm],), (Osb,))
        P.op("dve", lambda e: e.reciprocal(out=rz.ap[:T, 0:4], in_=Osb.ap[:T, :, 256]), (Osb,), (rz,))
        P.op("dve", lambda e: e.tensor_scalar(out=rz.ap[:T, 4:8], in0=rz.ap[:T, 0:4], scalar1=lam.ap[:T, 2:3], scalar2=None, op0=ALU.mult), (rz, lam), (rz,))
        sd = nst()
        P.op("act", lambda e: e.memzero(sd.ap[:]), (), (sd,))
        for h in range(2):
            P.op("dve", lambda e, h=h: e.tensor_scalar(out=tq.ap[:T, :], in0=Osb.ap[:T, 2 * h + 1, 0:256], scalar1=rz.ap[:T, 5 + 2 * h:6 + 2 * h], scalar2=None, op0=ALU.mult), (Osb, rz), (tq,))
            P.op("dve", lambda e, h=h: e.scalar_tensor_tensor(out=od.ap[:T, h, :], in0=Osb.ap[:T, 2 * h, 0:256], scalar=rz.ap[:T, 2 * h:2 * h + 1], in1=tq.ap[:T, :],
                                                             op0=ALU.mult, op1=ALU.add), (Osb, rz, tq), (od,))
            P.op("act", lambda e, h=h: e.activation(out=junk.ap[:T, 0:256], in_=od.ap[:T, h, :], func=AF.Square, accum_out=sd.ap[:T, h:h + 1]), (od,), (junk, sd))
        rstd_chain(sd, 0, 2, 1.0 / 256)
        for h in range(2):
            P.op("dve", lambda e, h=h: e.scalar_tensor_tensor(out=d_sb.ap[:T, h * 256:(h + 1) * 256], in0=od.ap[:T, h, :], scalar=sd.ap[:T, h:h + 1], in1=dnormB.ap[:T, :],
                                                             op0=ALU.mult, op1=ALU.mult), (od, sd, dnormB), (d_sb,))
        if adcol is not None:
            transposes(d_sb, T, 4, dT, lambda j, n: slice(j, j + n))
            P.dma("pool", ad[adcol[0], 4:8, :, adcol[1]:adcol[1] + T].rearrange("c p t -> p c t"), dT.ap[:, :, 0:T], (dT,), (), sembuf=dT)

    idx = 0
    try:
        tile(idx, "meta", 16, xp[0:16, :], 0, ok[0:16, :], ov[0:16, :], None, 0)
    except StopIteration:
        return dr, []
    idx += 1
    for i in range(1, NPT + 1):
        r0 = 16 + 128 * (i - 1)
        tile(idx, "prompt", 128, xp[r0:r0 + 128, :], i, ok[r0:r0 + 128, :], ov[r0:r0 + 128, :], ((i - 1) // 32, 128 * ((i - 1) % 32)), i)
        idx += 1
    S_out = Buf(None, "S_out")
    P.dma("sp", oS.rearrange("(c p) v -> p c v", p=128), S.ap[:], (S,), (S_out,), sembuf=S)
    for sq in range(NSEQ):
        P.dma("sp", S.ap[:], st_in[sq].rearrange("(c p) v -> p c v", p=128), (), (S,))
        P.op("pool", lambda e: e.tensor_copy(out=Sbf.ap[:], in_=S.ap[:]), (S,), (Sbf,))
        for g_ in range(2):
            P.dma("pool", ckb.ap[:], ck[sq, g_ * 512:(g_ + 1) * 512, :].rearrange("(b s) c -> s b c", s=128), (), (ckb,))
            for q_ in range(4):
                tb = nbank()
                tv = tbv(tb)
                def fn(e, q_=q_, tv=tv):
                    ins = None
                    for hm in range(4):
                        ins = e.transpose(out=tv[:, hm * 128:(hm + 1) * 128], in_=ckb.ap[:, q_, hm * 128:(hm + 1) * 128], identity=ident.ap[:, :])
                    return ins
                P.op("pe", fn, (ckb, ident), (tb,))
                copy(evac_eng(), KTg[g_].ap[:, q_, :], tv[:, 0:512], (tb,), (KTg[g_],))
        for g_ in range(2):
            for h in range(2):
                P.dma("pool", Vg[g_].ap[:, :, h * 258:h * 258 + 256], cv[sq, g_ * 512:(g_ + 1) * 512, h * 256:(h + 1) * 256].rearrange("(b s) v -> s b v", s=128),
                      (), (Vg[g_],))
        r0 = sq * 64
        tile(idx, "sample", 64, xs[r0:r0 + 64, :], 129, oks[r0:r0 + 64, :], ovs[r0:r0 + 64, :], (sq // 4, 4096 + 64 * (sq % 4)), None, seq=sq)
        idx += 1
        P.dma("sp", oSs[sq].rearrange("(c p) v -> p c v", p=128), S.ap[:], (S,), (S_out,), sembuf=S)
    outs_bufs = [S, aT, dT] + kf + vf + KTc + Vc
    return dr, outs_bufs


NT3_FULL = 34
ADCOLS = 4096 + 256


def build_phase3(nc, P, NT3=NT3_FULL, adq_ap=None, NSUB=4, prefix=""):
    dr = {}
    def din(name, shape, dt=F32):
        dr[name] = nc.dram_tensor(prefix + name, list(shape), dt, kind="ExternalInput").ap()
        return dr[name]
    x3 = din("x3", [NT3 * 128, 2048])
    wga = din("wga", [2048, 2048])
    wgb = din("wgb", [2048, 2048])
    wbra = din("wbra", [2048, 2048])
    wbrd = din("wbrd", [2048, 2048])
    wo = din("wo", [2048, 2048])
    wf1 = din("wf1", [2048, 8192])
    wf2 = din("wf2", [8192, 2048])
    gains = din("gains", [4, 2048])
    if adq_ap is None:
        adq_ap = din("adq", [4, 8, 128, ADCOLS], BF16)
    y3 = nc.dram_tensor(prefix + "y3", [NT3 * 128, 2048], F32, kind="ExternalOutput").ap()
    dr["y3"] = y3
    wg_s = nc.dram_tensor(prefix + "wg_s", [16, 128, 4, 16, 128], BF16).ap()
    wo_s = nc.dram_tensor(prefix + "wo_s", [4, 128, 16, 512], BF16).ap()
    w1_s = nc.dram_tensor(prefix + "w1_s", [16, 128, 4, 16, 128], BF16).ap()
    w2_s = nc.dram_tensor(prefix + "w2_s", [4, 4, 128, 16, 512], BF16).ap()
    wgs_b = [Buf(None, f"wgs{i}") for i in range(16)]
    wos_b = [Buf(None, f"wos{i}") for i in range(4)]
    w1s_b = [Buf(None, f"w1s{i}") for i in range(16)]
    w2s_b = [Buf(None, f"w2s{i}") for i in range(16)]
    cvsem = Buf(None, "cv")
    for n in range(16):
        for m, w in enumerate((wga, wgb, wbra, wbrd)):
            P.dma("pool", wg_s[n, :, m, :, :], w[:, n * 128:(n + 1) * 128].rearrange("(kh p) n -> p kh n", p=128), (), (wgs_b[n],), sembuf=cvsem)
    for cg in range(4):
        for kq in range(4):
            P.dma("pool", wo_s[cg, :, kq * 4:(kq + 1) * 4, :], wo[kq * 512:(kq + 1) * 512, cg * 512:(cg + 1) * 512].rearrange("(kh p) n -> p kh n", p=128), (), (wos_b[cg],), sembuf=cvsem)
    for hg in range(16):
        for hc in range(4):
            c0 = (hg * 4 + hc) * 128
            P.dma("pool", w1_s[hg, :, hc, :, :], wf1[:, c0:c0 + 128].rearrange("(kh p) n -> p kh n", p=128), (), (w1s_b[hg],), sembuf=cvsem)
    for cg in range(4):
        for pc in range(4):
            for kq in range(4):
                r0 = pc * 2048 + kq * 512
                P.dma("pool", w2_s[cg, pc, :, kq * 4:(kq + 1) * 4, :], wf2[r0:r0 + 512, cg * 512:(cg + 1) * 512].rearrange("(kh p) n -> p kh n", p=128), (), (w2s_b[cg * 4 + pc],), sembuf=cvsem)

    cv_t = ("dma", cvsem.sem["pool"][0], cvsem.sem["pool"][1])
    for b_ in wgs_b + wos_b + w1s_b + w2s_b:
        b_.wr = cv_t
    xh = P.sb("xh", [128, NSUB, 2048], F32)
    y1 = P.sb("y1", [128, NSUB, 2048], F32)
    xn = P.sb("xn3", [128, 2048], BF16)
    uT = P.sb("uT3", [128, 16, NSUB * 128], BF16)
    big = P.sb("big", [128, 64, NSUB * 128], BF16)
    WS = [P.sb("WS", [128, 8192], BF16) for _ in range(2)]
    gB = [P.sb("gB", [128, 2048], F32) for _ in range(2)]
    eg = [P.sb("eg", [128, 512], F32) for _ in range(2)]
    rl = [P.sb("rl", [128, 512], F32) for _ in range(2)]
    ident = P.sb("ident3", [128, 128], BF16)
    cf = P.sb("cf3", [128, 128], F32)
    st8 = [P.sb("st83", [128, 8], F32, small=True) for _ in range(8)]
    banks = [P.ps("bk3", [128, 512], F32) for _ in range(8)]
    state = {"bk": 0, "st": 0, "ev": 0, "ws": 0, "g": 0}

    def nbank():
        b = banks[state["bk"] % 8]
        state["bk"] += 1
        return b

    def nst():
        b = st8[state["st"] % 8]
        state["st"] += 1
        return b

    def nws():
        b = WS[state["ws"] % 2]
        state["ws"] += 1
        return b

    def evac_eng():
        state["ev"] += 1
        return "act" if state["ev"] % 2 else "dve"

    def copy(eng, out_ap, in_ap, reads, writes):
        if eng == "act":
            return P.op("act", lambda e: e.copy(out=out_ap, in_=in_ap), reads, writes)
        return P.op("dve", lambda e: e.tensor_copy(out=out_ap, in_=in_ap), reads, writes)

    P.op("pool", lambda e: e.memset(cf.ap[:], 0.0), (), (cf,))
    P.op("pool", lambda e: e.affine_select(out=cf.ap[:], in_=cf.ap[:], pattern=[[-1, 128]], compare_op=ALU.not_equal, fill=1.0, base=0, channel_multiplier=1), (cf,), (cf,))
    P.op("pool", lambda e: e.tensor_copy(out=ident.ap[:], in_=cf.ap[:]), (cf,), (ident,))

    def rstd_chain(ssb, n, scale):
        P.op("act", lambda e: e.activation(out=ssb.ap[:, 0:n], in_=ssb.ap[:, 0:n], func=AF.Ln, scale=scale, bias=EPS), (ssb,), (ssb,))
        P.op("act", lambda e: e.activation(out=ssb.ap[:, 0:n], in_=ssb.ap[:, 0:n], func=AF.Exp, scale=-0.5), (ssb,), (ssb,))

    def load_gain(i):
        g = gB[state["g"] % 2]
        state["g"] += 1
        P.dma("sp", g.ap[:], gains[i:i + 1, :].partition_broadcast(128), (), (g,))
        return g

    def norm_to_uT(src, nsub, gi):
        g = load_gain(gi)
        ss = nst()
        P.op("act", lambda e: e.memzero(ss.ap[:]), (), (ss,))
        for s_ in range(nsub):
            P.op("act", lambda e, s_=s_: e.activation(out=xn.ap[:, :], in_=src.ap[:, s_, :], func=AF.Square, accum_out=ss.ap[:, s_:s_ + 1]), (src,), (xn, ss))
        rstd_chain(ss, nsub, 1.0 / 2048)
        for s_ in range(nsub):
            P.op("dve", lambda e, s_=s_: e.scalar_tensor_tensor(out=xn.ap[:, :], in0=src.ap[:, s_, :], scalar=ss.ap[:, s_:s_ + 1], in1=g.ap[:, :], op0=ALU.mult, op1=ALU.mult), (src, ss, g), (xn,))
            for j0 in (0, 8):
                tb = nbank()
                tv = tb.ap[:, :].bitcast(BF16)
                def fn(e, j0=j0, tv=tv):
                    ins = None
                    for q in range(8):
                        ins = e.transpose(out=tv[:, q * 128:(q + 1) * 128], in_=xn.ap[:, (j0 + q) * 128:(j0 + q + 1) * 128], identity=ident.ap[:, :])
                    return ins
                P.op("pe", fn, (xn, ident), (tb,))
                copy(evac_eng(), uT.ap[:, j0:j0 + 8, s_ * 128:(s_ + 1) * 128], tv[:, :].rearrange("p (q t) -> p q t", q=8), (tb,), (uT,))

    def post_norm_residual(nsub, gi, dst_is_y1):
        g = load_gain(gi)
        ss = nst()
        P.op("act", lambda e: e.memzero(ss.ap[:]), (), (ss,))
        for s_ in range(nsub):
            P.op("act", lambda e, s_=s_: e.activation(out=xn.ap[:, :], in_=y1.ap[:, s_, :], func=AF.Square, accum_out=ss.ap[:, s_:s_ + 1]), (y1,), (xn, ss))
        rstd_chain(ss, nsub, 1.0 / 2048)
        for s_ in range(nsub):
            P.op("dve", lambda e, s_=s_: e.scalar_tensor_tensor(out=y1.ap[:, s_, :], in0=y1.ap[:, s_, :], scalar=ss.ap[:, s_:s_ + 1], in1=g.ap[:, :], op0=ALU.mult, op1=ALU.mult), (y1, ss, g), (y1,))
            if dst_is_y1:
                P.op("pool", lambda e, s_=s_: e.tensor_tensor(out=y1.ap[:, s_, :], in0=y1.ap[:, s_, :], in1=xh.ap[:, s_, :], op=ALU.add), (y1, xh), (y1,))
            else:
                P.op("pool", lambda e, s_=s_: e.tensor_tensor(out=xh.ap[:, s_, :], in0=y1.ap[:, s_, :], in1=xh.ap[:, s_, :], op=ALU.add), (y1, xh), (xh,))

    def supertile(t0, nsub):
        TT = nsub * 128
        c0 = t0 * 128
        P.dma("sp", xh.ap[:, 0:nsub, :], x3[c0:c0 + TT, :].rearrange("(s p) d -> p s d", p=128), (), (xh,))
        for r_ in range(4):
            P.dma("sp", big.ap[:, r_ * 4:r_ * 4 + 4, 0:TT], adq_ap[r_, 0:4, :, c0:c0 + TT].rearrange("c p t -> p c t"), (), (big,))
            P.dma("sp", big.ap[:, 16 + r_ * 4:16 + r_ * 4 + 4, 0:TT], adq_ap[r_, 4:8, :, c0:c0 + TT].rearrange("c p t -> p c t"), (), (big,))
        norm_to_uT(xh, nsub, 0)
        for n in range(16):
            ws = nws()
            P.dma("sp", ws.ap[:, :], wg_s[n].rearrange("p m k n -> p (m k n)"), (wgs_b[n],), (ws,))
            wv = ws.ap[:, :].rearrange("p (m k n) -> p m k n", m=4, k=16)
            bk = [nbank() for _ in range(4)]
            for m in range(4):
                rb, roff = (uT, 0) if m < 2 else ((big, 0) if m == 2 else (big, 16))
                def fn(e, m=m, rb=rb, roff=roff, bk=bk, wv=wv):
                    ins = None
                    for k in range(16):
                        ins = e.matmul(bk[m].ap[:, 0:TT], lhsT=wv[:, m, k, :], rhs=rb.ap[:, roff + k, 0:TT], start=(k == 0), stop=(k == 15))
                    return ins
                P.op("pe", fn, (ws, rb), (bk[m],))
            for m in range(2):
                P.op("act", lambda e, m=m, bk=bk: e.activation(out=eg[m].ap[:, 0:TT], in_=bk[m].ap[:, 0:TT], func=AF.Exp, scale=-1.0), (bk[m],), (eg[m],))
                P.op("dve", lambda e, m=m: e.tensor_scalar_add(out=eg[m].ap[:, 0:TT], in0=eg[m].ap[:, 0:TT], scalar1=1.0), (eg[m],), (eg[m],))
                P.op("dve", lambda e, m=m: e.reciprocal(out=eg[m].ap[:, 0:TT], in_=eg[m].ap[:, 0:TT]), (eg[m],), (eg[m],))
                P.op("dve", lambda e, m=m, bk=bk: e.tensor_tensor(out=eg[m].ap[:, 0:TT], in0=bk[2 + m].ap[:, 0:TT], in1=eg[m].ap[:, 0:TT], op=ALU.mult), (bk[2 + m], eg[m]), (eg[m],))
            P.op("pool", lambda e, n=n: e.tensor_tensor(out=big.ap[:, 32 + n, 0:TT], in0=eg[0].ap[:, 0:TT], in1=eg[1].ap[:, 0:TT], op=ALU.add), (eg[0], eg[1]), (big,))
        for cg in range(4):
            ws = nws()
            P.dma("sp", ws.ap[:, :], wo_s[cg].rearrange("p k n -> p (k n)"), (wos_b[cg],), (ws,))
            wv = ws.ap[:, :].rearrange("p (k n) -> p k n", k=16)
            for s_ in range(nsub):
                bk = nbank()
                def fn(e, s_=s_, bk=bk, wv=wv):
                    ins = None
                    for k in range(16):
                        ins = e.matmul(bk.ap[:, :], lhsT=big.ap[:, 32 + k, s_ * 128:(s_ + 1) * 128], rhs=wv[:, k, :], start=(k == 0), stop=(k == 15))
                    return ins
                P.op("pe", fn, (ws, big), (bk,))
                copy(evac_eng(), y1.ap[:, s_, cg * 512:(cg + 1) * 512], bk.ap[:, :], (bk,), (y1,))
        post_norm_residual(nsub, 1, False)
        norm_to_uT(xh, nsub, 2)
        for hg in range(16):
            ws = nws()
            P.dma("sp", ws.ap[:, :], w1_s[hg].rearrange("p c k n -> p (c k n)"), (w1s_b[hg],), (ws,))
            wv = ws.ap[:, :].rearrange("p (c k n) -> p c k n", c=4, k=16)
            for hc in range(4):
                bk = nbank()
                def fn(e, hc=hc, bk=bk, wv=wv):
                    ins = None
                    for k in range(16):
                        ins = e.matmul(bk.ap[:, 0:TT], lhsT=wv[:, hc, k, :], rhs=uT.ap[:, k, 0:TT], start=(k == 0), stop=(k == 15))
                    return ins
                P.op("pe", fn, (ws, uT), (bk,))
                r_ = rl[(hg * 4 + hc) % 2]
                P.op("act", lambda e, bk=bk, r_=r_: e.activation(out=r_.ap[:, 0:TT], in_=bk.ap[:, 0:TT], func=AF.Relu), (bk,), (r_,))
                P.op("pool", lambda e, r_=r_, hg=hg, hc=hc: e.tensor_tensor(out=big.ap[:, hg * 4 + hc, 0:TT], in0=r_.ap[:, 0:TT], in1=r_.ap[:, 0:TT], op=ALU.mult), (r_,), (big,))
        for cg in range(4):
            acc = [nbank() for _ in range(nsub)]
            for pc in range(4):
                ws = nws()
                P.dma("sp", ws.ap[:, :], w2_s[cg, pc].rearrange("p k n -> p (k n)"), (w2s_b[cg * 4 + pc],), (ws,))
                wv = ws.ap[:, :].rearrange("p (k n) -> p k n", k=16)
                for s_ in range(nsub):
                    def fn(e, s_=s_, pc=pc, acc=acc, wv=wv):
                        ins = None
                        for k in range(16):
                            ins = e.matmul(acc[s_].ap[:, :], lhsT=big.ap[:, pc * 16 + k, s_ * 128:(s_ + 1) * 128], rhs=wv[:, k, :], start=(pc == 0 and k == 0), stop=(pc == 3 and k == 15))
                        return ins
                    P.op("pe", fn, (ws, big), (acc[s_],))
            for s_ in range(nsub):
                copy(evac_eng(), y1.ap[:, s_, cg * 512:(cg + 1) * 512], acc[s_].ap[:, :], (acc[s_],), (y1,))
        post_norm_residual(nsub, 3, True)
        P.dma("sp", y3[c0:c0 + TT, :].rearrange("(s p) d -> p s d", p=128), y1.ap[:, 0:nsub, :], (y1,), (), sembuf=y1)
    t0 = 0
    while t0 < NT3:
        nsub = min(NSUB, NT3 - t0)
        supertile(t0, nsub)
        t0 += nsub
    return dr


MODE = "two"


def host_inputs_p1(ins, s, g, NPT, NSEQ):
    LP = 16 + 128 * NPT
    w_in = ins["w_in"][0]
    offs = np.cumsum([0, 1024, 1024, 2048, 16, 2048, 2048, 2048, 2048, 2048, 2048])
    o_gq, o_gk, o_gv, o_al, o_r, o_dq, o_dk, o_dv, o_ga, o_gb = offs[:10]
    cols = np.concatenate([
        np.arange(o_gq + g * 256, o_gq + (g + 1) * 256), np.arange(o_gk + g * 256, o_gk + (g + 1) * 256),
        np.arange(o_gv + g * 512, o_gv + (g + 1) * 512), np.arange(o_r + g * 512, o_r + (g + 1) * 512),
        np.arange(o_dq + g * 512, o_dq + (g + 1) * 512), np.arange(o_dk + g * 512, o_dk + (g + 1) * 512),
        np.arange(o_dv + g * 512, o_dv + (g + 1) * 512), np.arange(o_al, o_al + 16)])
    d = {}
    d["xp"] = np.ascontiguousarray(np.concatenate([ins["meta"], ins["x_prompt"][s, :LP - 16]], axis=0))
    d["xs"] = np.ascontiguousarray(ins["x_sample"][16 * s:16 * s + NSEQ].reshape(NSEQ * 64, 2048))
    d["w1"] = np.ascontiguousarray(w_in[:, cols])
    d["a2b"] = np.ascontiguousarray(np.concatenate([ins["w_gla_a2"][0][:, g * 256:(g + 1) * 256], ins["b_gla_a"][0][None, g * 256:(g + 1) * 256]], axis=0))
    d["gnorm"] = np.ascontiguousarray(ins["gla_norm"][0][None, :])
    d["dnorm"] = np.ascontiguousarray(ins["diff_norm"][0][None, :])
    d["nmp"] = np.ascontiguousarray(ins["norm_mix_pre"][0].reshape(16, 128).T)
    d["lql"] = np.ascontiguousarray(np.concatenate([ins["diff_lq1"][0], ins["diff_lk1"][0], ins["diff_lq2"][0], ins["diff_lk2"][0]])[None, :])
    d["ck"] = np.ascontiguousarray(ins["cache_k"][0, 16 * s:16 * s + NSEQ, :, 2 * g:2 * g + 2].reshape(NSEQ, 1024, 512))
    d["cv"] = np.ascontiguousarray(ins["cache_v"][0, 16 * s:16 * s + NSEQ, :, 2 * g:2 * g + 2].reshape(NSEQ, 1024, 512))
    d["st_in"] = np.ascontiguousarray(ins["state_gla"][0, 16 * s:16 * s + NSEQ, g])
    return d


def host_inputs_p3(ins, s, g, prefix=""):
    w_in = ins["w_in"][0]
    d = {}
    d["wga"] = np.ascontiguousarray(w_in[:, 12304:14352])
    d["wgb"] = np.ascontiguousarray(w_in[:, 14352:16400])
    d["wbra"] = np.ascontiguousarray(ins["w_br_gla"][0])
    d["wbrd"] = np.ascontiguousarray(ins["w_br_diff"][0])
    d["wo"] = np.ascontiguousarray(ins["w_o"][0])
    d["wf1"] = np.ascontiguousarray(ins["w_ff1"][0])
    d["wf2"] = np.ascontiguousarray(ins["w_ff2"][0])
    d["gains"] = np.ascontiguousarray(np.stack([ins["norm_mix_pre"][0], ins["norm_mix_post"][0], ins["norm_ffn_pre"][0], ins["norm_ffn_post"][0]]))
    d["x3"] = np.ascontiguousarray(np.concatenate([ins["x_prompt"][s, 4096 * g:4096 * (g + 1)],
                                                   ins["x_sample"][16 * s + 4 * g:16 * s + 4 * g + 4].reshape(256, 2048)], axis=0))
    return {prefix + k: v for k, v in d.items()}


def _assemble(ins, res1, y3s):
    f = np.float32
    y_prompt = np.zeros((2, 16384, 2048), f)
    y_sample = np.zeros((32, 64, 2048), f)
    nkp = np.zeros((1, 2, 16400, 8, 2, 128), f)
    nvp = np.zeros((1, 2, 16400, 8, 256), f)
    nsp = np.zeros((1, 2, 4, 256, 512), f)
    nks = np.zeros((1, 32, 64, 8, 2, 128), f)
    nvs = np.zeros((1, 32, 64, 8, 256), f)
    nss = np.zeros((1, 32, 4, 256, 512), f)
    for c in range(8):
        s, g = c // 4, c % 4
        r = res1[c]
        nkp[0, s, :, 2 * g:2 * g + 2] = np.asarray(r["ok"]).reshape(16400, 2, 2, 128)
        nvp[0, s, :, 2 * g:2 * g + 2] = np.asarray(r["ov"]).reshape(16400, 2, 256)
        nsp[0, s, g] = np.asarray(r["oS"])
        nks[0, 16 * s:16 * s + 16, :, 2 * g:2 * g + 2] = np.asarray(r["oks"]).reshape(16, 64, 2, 2, 128)
        nvs[0, 16 * s:16 * s + 16, :, 2 * g:2 * g + 2] = np.asarray(r["ovs"]).reshape(16, 64, 2, 256)
        nss[0, 16 * s:16 * s + 16, g] = np.asarray(r["oSs"])
        y3 = np.asarray(y3s[c])
        y_prompt[s, 4096 * g:4096 * (g + 1)] = y3[:4096]
        y_sample[16 * s + 4 * g:16 * s + 4 * g + 4] = y3[4096:].reshape(4, 64, 2048)
    return (y_prompt, y_sample, nkp, nvp, nsp, nks, nvs, nss)


def kernel(**inputs):
    ins = {k: np.asarray(v) for k, v in inputs.items()}
    if MODE == "two":
        nc1 = bass.Bass("TRN2", target_bir_lowering=False)
        with ExitStack() as es:
            P = Prog(nc1, es)
            build_phase1(nc1, P, 128, 16, ad_kind="ExternalOutput")
            P.final_all("sp")
            with nc1.Block() as block:
                P.emit(block)
        maps1 = [host_inputs_p1(ins, c // 4, c % 4, 128, 16) for c in range(8)]
        res1 = run_bass_kernel_spmd(nc1, maps1, core_ids=list(range(8))).results
        del maps1
        nc3 = bass.Bass("TRN2", target_bir_lowering=False)
        with ExitStack() as es:
            P = Prog(nc3, es)
            build_phase3(nc3, P, 34)
            P.final_all("sp")
            with nc3.Block() as block:
                P.emit(block)
        maps3 = []
        for c in range(8):
            s, g = c // 4, c % 4
            m = host_inputs_p3(ins, s, g)
            m["adq"] = np.ascontiguousarray(np.stack([np.asarray(res1[s * 4 + r]["ad"])[g] for r in range(4)]))
            maps3.append(m)
        res3 = run_bass_kernel_spmd(nc3, maps3, core_ids=list(range(8))).results
        return _assemble(ins, res1, [r["y3"] for r in res3])
    raise NotImplementedError
```
